# Optimizing a Trainium2 kernel written in Bass

```python
import jax, jax.numpy as jnp
from jax import lax
import numpy as np

D_MODEL = 1024
BATCH = 8
SEQ = 2048
DEPTH = 4
DEC_BATCH = 16
DEC_SEQ = 4096
PAST_LEN = 128

MIX_WIDTH = D_MODEL
D_CONV = MIX_WIDTH // 2
D_FFT = MIX_WIDTH - D_CONV
HEAD_DIM = 64
N_CONV_HEADS = D_CONV // HEAD_DIM
N_FFT_GROUPS = D_FFT // HEAD_DIM
CONV_WIDTH = 3
IN_WIDTH = 3 * D_CONV + D_FFT
D_FF = ((8 * D_MODEL // 3 + 255) // 256) * 256
N_MOD = 6
EPS = 1e-6

kernel_name = "hybrid_conv_fourier_encoder"


def rms_norm(x, g):
    xf = x.astype(jnp.float32)
    y = xf * lax.rsqrt(jnp.mean(xf * xf, axis=-1, keepdims=True) + EPS)
    return (y * g.astype(jnp.float32)).astype(x.dtype)


def centred_depthwise_conv3(u, w, b):
    up = jnp.pad(u, ((0, 0), (1, 1), (0, 0)))
    return up[:, :-2] * w[0] + up[:, 1:-1] * w[1] + up[:, 2:] * w[2] + b


def fourier_groups(f):
    bsz, seq, _ = f.shape
    fg = f.reshape(bsz, seq, N_FFT_GROUPS, HEAD_DIM).astype(jnp.float32)
    out = jnp.fft.fft2(fg, axes=(1, 3)).real
    return out.reshape(bsz, seq, D_FFT).astype(f.dtype)


def run_trunk(x, c, w_ada, b_ada, g_pre_mix, g_post_mix, w_in, conv_w, conv_b,
              g_conv, g_fft, w_out, g_pre_ffn, g_post_ffn, w_gate, w_up, w_down):
    c_act = jax.nn.silu(c)
    for l in range(DEPTH):
        mod = c_act @ w_ada[l] + b_ada[l]
        sh_m, sc_m, gt_m, sh_f, sc_f, gt_f = [m[:, None, :] for m in jnp.split(mod, N_MOD, axis=-1)]

        h = rms_norm(x, g_pre_mix[l]) * (1.0 + sc_m) + sh_m
        z = h @ w_in[l]
        bg = z[..., :D_CONV]
        cg = z[..., D_CONV:2 * D_CONV]
        v = z[..., 2 * D_CONV:3 * D_CONV]
        f = z[..., 3 * D_CONV:]
        conv_out = bg * centred_depthwise_conv3(cg * v, conv_w[l], conv_b[l])
        fft_out = fourier_groups(f)
        merged = jnp.concatenate([rms_norm(conv_out, g_conv[l]), rms_norm(fft_out, g_fft[l])], axis=-1)
        o = merged @ w_out[l]
        x = x + gt_m * rms_norm(o, g_post_mix[l])

        h = rms_norm(x, g_pre_ffn[l]) * (1.0 + sc_f) + sh_f
        ff = (jax.nn.silu(h @ w_gate[l]) * (h @ w_up[l])) @ w_down[l]
        x = x + gt_f * rms_norm(ff, g_post_ffn[l])
    return x


def setup_inputs(seed: int = 0) -> dict:
    key = jax.random.key(seed)
    ks = jax.random.split(key, 20)
    f32 = jnp.float32
    nrm = lambda k, s, scale: jax.random.normal(k, s, f32) * scale
    gain = lambda k, s: 1.0 + 0.05 * jax.random.normal(k, s, f32)
    return {
        "x_prompt": nrm(ks[0], (BATCH, SEQ, D_MODEL), 1.0),
        "x_sample": nrm(ks[1], (DEC_BATCH, DEC_SEQ, D_MODEL), 1.0),
        "c_prompt": nrm(ks[2], (BATCH, D_MODEL), 1.0),
        "c_sample": nrm(ks[3], (DEC_BATCH, D_MODEL), 1.0),
        "w_ada": nrm(ks[4], (DEPTH, D_MODEL, N_MOD * D_MODEL), 0.3 * D_MODEL ** -0.5),
        "b_ada": nrm(ks[5], (DEPTH, N_MOD * D_MODEL), 0.02),
        "g_pre_mix": gain(ks[6], (DEPTH, D_MODEL)),
        "g_post_mix": gain(ks[7], (DEPTH, D_MODEL)),
        "w_in": nrm(ks[8], (DEPTH, D_MODEL, IN_WIDTH), D_MODEL ** -0.5),
        "conv_w": nrm(ks[9], (DEPTH, CONV_WIDTH, D_CONV), CONV_WIDTH ** -0.5),
        "conv_b": nrm(ks[10], (DEPTH, D_CONV), 0.01),
        "g_conv": gain(ks[11], (DEPTH, D_CONV)),
        "g_fft": gain(ks[12], (DEPTH, D_FFT)),
        "w_out": nrm(ks[13], (DEPTH, MIX_WIDTH, D_MODEL), MIX_WIDTH ** -0.5),
        "g_pre_ffn": gain(ks[14], (DEPTH, D_MODEL)),
        "g_post_ffn": gain(ks[15], (DEPTH, D_MODEL)),
        "w_gate": nrm(ks[16], (DEPTH, D_MODEL, D_FF), D_MODEL ** -0.5),
        "w_up": nrm(ks[17], (DEPTH, D_MODEL, D_FF), D_MODEL ** -0.5),
        "w_down": nrm(ks[18], (DEPTH, D_FF, D_MODEL), D_FF ** -0.5),
    }


def reference(x_prompt, x_sample, c_prompt, c_sample, w_ada, b_ada, g_pre_mix, g_post_mix,
              w_in, conv_w, conv_b, g_conv, g_fft, w_out, g_pre_ffn, g_post_ffn,
              w_gate, w_up, w_down):
    y_prompt = run_trunk(x_prompt, c_prompt, w_ada, b_ada, g_pre_mix, g_post_mix, w_in, conv_w,
                         conv_b, g_conv, g_fft, w_out, g_pre_ffn, g_post_ffn, w_gate, w_up, w_down)
    y_sample = run_trunk(x_sample, c_sample, w_ada, b_ada, g_pre_mix, g_post_mix, w_in, conv_w,
                         conv_b, g_conv, g_fft, w_out, g_pre_ffn, g_post_ffn, w_gate, w_up, w_down)
    return (y_prompt, y_sample)
```

```python
import contextlib
import numpy as np
import ml_dtypes
import concourse.bass as bass
import concourse.mybir as mybir
from concourse.bass_utils import run_bass_kernel_spmd

F32 = mybir.dt.float32
BF16 = mybir.dt.bfloat16
AF = mybir.ActivationFunctionType
ALU = mybir.AluOpType

D = 1024
KC = 8
DFF = 2816
FJ = 22
DH = 512
G = 256
NMOD = 6
EPS = 1e-6
ENGS = ("pe", "act", "dve", "pool", "sp")


class Op:
    __slots__ = ("eng", "fn", "deps", "need", "tok", "waits", "dma")


class Sched:
    def __init__(self):
        self.ops = []
        self.lastw = {}
        self.readers = {}
        self.pending = {e: [] for e in ENGS}
        self.last_by_key = {}

    def add(self, eng, fn, reads=(), writes=(), dma=None, follow=False):
        op = Op()
        op.eng = eng
        op.fn = fn
        op.dma = dma
        op.need = False
        op.tok = None
        deps = {}
        for r in reads:
            d = self.lastw.get(r)
            if d is not None:
                deps[id(d)] = d
        for w in writes:
            if follow:
                continue
            d = self.lastw.get(w)
            if d is not None:
                deps[id(d)] = d
            rd = self.readers.get(w)
            if rd:
                for d in rd.values():
                    deps[id(d)] = d
        for d in self.pending[eng]:
            deps[id(d)] = d
        self.pending[eng] = []
        key = ("dma", dma) if dma else eng
        for r in reads:
            self.readers.setdefault(r, {})[key] = op
        for w in writes:
            self.lastw[w] = op
            self.readers[w] = {}
        op.deps = list(deps.values())
        self.ops.append(op)
        self.last_by_key[key] = op
        return op

    def barrier(self):
        lst = list(self.last_by_key.values())
        for e in ENGS:
            self.pending[e] = list(lst)

    @staticmethod
    def _skip(d, op):
        return d.dma is None and op.dma is None and d.eng == "pe" and op.eng == "pe"

    def finalize(self):
        for op in self.ops:
            for d in op.deps:
                if not self._skip(d, op):
                    d.need = True
        cnt = {}
        seen = {e: {} for e in ENGS}
        for op in self.ops:
            waits = {}
            sn = seen[op.eng]
            for d in op.deps:
                if self._skip(d, op):
                    continue
                key = ("dma", d.dma) if d.dma else d.eng
                if sn.get(key, 0) >= d.tok:
                    continue
                if waits.get(key, 0) < d.tok:
                    waits[key] = d.tok
            sn.update(waits)
            op.waits = waits
            if op.dma:
                k = ("dma", op.dma)
                cnt[k] = cnt.get(k, 0) + 16
                op.tok = cnt[k]
            elif op.need:
                cnt[op.eng] = cnt.get(op.eng, 0) + 1
                op.tok = cnt[op.eng]
        self.final_counts = cnt

    def sem_keys(self):
        keys = list(ENGS)
        seen = set()
        for op in self.ops:
            if op.dma and op.dma not in seen:
                seen.add(op.dma)
                keys.append(("dma", op.dma))
        return keys

    def emit_engine(self, name, e, sems):
        for op in self.ops:
            if op.eng != name:
                continue
            for key, val in op.waits.items():
                e.wait_ge(sems[key], val)
            ins = op.fn(e)
            if op.dma:
                ins.then_inc(sems[("dma", op.dma)], 16)
            elif op.need:
                ins.then_inc(sems[op.eng], 1)


def _prod(xs):
    r = 1
    for x in xs:
        r *= x
    return r


def build_program(seqs, L, do_layers=True):
    NS = len(seqs)
    T = sum(seqs)
    T0s = [sum(seqs[:i]) for i in range(NS)]
    SMAX = max(seqs)
    nc = bass.Bass("TRN2", target_bir_lowering=False)

    def din(name, shape, dt):
        return nc.dram_tensor(name, list(shape), dt, kind="ExternalInput").ap()

    xin = din("xin", [T, D], F32)
    cT_d = din("cT", [128, KC, 4], F32)
    ident_d = din("ident", [128, 128], F32)
    ones_d = din("ones", [128, 128], BF16)
    chan_d = din("chan", [128, 2, 128], BF16)
    badaT_d = din("badaT", [128, L, 48], F32)
    gvec_d = din("gvec", [128, 4, L, KC], F32)
    convp_d = din("convp", [128, L, 6, 4], F32)
    w_ada = din("w_ada", [L, D, NMOD * D], F32)
    w_in = din("w_in", [L, D, 4 * DH], F32)
    w_out = din("w_out", [L, D, D], F32)
    w_gate = din("w_gate", [L, D, DFF], F32)
    w_up = din("w_up", [L, D, DFF], F32)
    w_down = din("w_down", [L, DFF, D], F32)
    tabs = {}
    for S in sorted(set(seqs)):
        nth = S // 256
        ntp = min(4, nth)
        tabs[S] = din("tab%d" % S, [S // G, nth // ntp, 128, ntp * 512], BF16)
    yout = nc.dram_tensor("yout", [T, D], F32, kind="ExternalOutput").ap()
    xT = nc.dram_tensor("xT", [KC, 128, T], F32).ap()

    ARENA_W = 53120
    es = contextlib.ExitStack()
    arena = es.enter_context(nc.sbuf_tensor("arena", [128, ARENA_W], F32))
    ps = es.enter_context(nc.psum_tensor("ps", [128, 8, 512], F32))

    def V(off, shape, dt):
        n = _prod(shape[1:])
        esz = 4 if dt == F32 else 2
        nb = n * esz
        assert off % 4 == 0 and nb % 4 == 0 and off + nb <= ARENA_W * 4, (off, nb)
        ap = arena[:, off // 4:(off + nb) // 4]
        if dt != F32:
            ap = ap.bitcast(dt)
        if len(shape) == 3:
            ap = ap.rearrange("p (a b) -> p a b", a=shape[1])
        elif len(shape) == 4:
            ap = ap.rearrange("p (a b c) -> p a b c", a=shape[1], b=shape[2])
        elif len(shape) == 5:
            ap = ap.rearrange("p (a b c d) -> p a b c d", a=shape[1], b=shape[2], c=shape[3])
        return ap

    class Carver:
        def __init__(self, base):
            self.off = base

        def __call__(self, shape, dt):
            n = _prod(shape[1:]) * (4 if dt == F32 else 2)
            n = (n + 31) // 32 * 32
            v = V(self.off, shape, dt)
            self.off += n
            return v

    cv = Carver(0)
    ones_sb = cv([128, 128], BF16)
    ident_sb = cv([128, 128], F32)
    chan_sb = cv([128, 2, 128], BF16)
    cact = cv([128, KC, 4], F32)
    modT = cv([128, L, 48, 4], F32)
    badaT = cv([128, L, 48], F32)
    gvec = cv([128, 4, L, KC], F32)
    convp = cv([128, L, 6, 4], F32)
    PP = cv([128, L, NS, 4, KC], F32)
    CONST_END = (cv.off + 255) // 256 * 256
    assert CONST_END <= 8192, CONST_END
    BASE = 8192

    S_ = Sched()
    R_ALL = "const"

    def dma(q, out, in_, reads, writes, sem, follow=False):
        S_.add(q, lambda e: e.dma_start(out=out, in_=in_), reads, writes, dma=sem, follow=follow)

    def mm(out, lhsT, rhs, start, stop, reads, writes):
        S_.add("pe", lambda e: e.matmul(out, lhsT=lhsT, rhs=rhs, start=start, stop=stop), reads, writes)

    def act(out, in_, func, reads, writes, scale=None, bias=None):
        kw = {}
        if scale is not None:
            kw["scale"] = scale
        if bias is not None:
            kw["bias"] = bias
        S_.add("act", lambda e: e.activation(out=out, in_=in_, func=func, **kw), reads, writes)

    def copy(eng, out, in_, reads, writes):
        if eng == "act":
            act(out, in_, AF.Copy, reads, writes)
        else:
            S_.add(eng, lambda e: e.tensor_copy(out=out, in_=in_), reads, writes)

    def tt(eng, out, in0, in1, op, reads, writes):
        S_.add(eng, lambda e: e.tensor_tensor(out=out, in0=in0, in1=in1, op=op), reads, writes)

    def stt(out, in0, scalar, in1, op0, op1, reads, writes):
        S_.add("dve", lambda e: e.scalar_tensor_tensor(out=out, in0=in0, scalar=scalar, in1=in1, op0=op0, op1=op1),
               reads, writes)

    def ts(eng, out, in0, s1, s2, op0, op1, reads, writes):
        if s2 is None:
            S_.add(eng, lambda e: e.tensor_scalar(out=out, in0=in0, scalar1=s1, scalar2=None, op0=op0), reads, writes)
        else:
            S_.add(eng, lambda e: e.tensor_scalar(out=out, in0=in0, scalar1=s1, scalar2=s2, op0=op0, op1=op1),
                   reads, writes)

    def memset(eng, out, val, writes):
        S_.add(eng, lambda e: e.memset(out, val), (), writes)

    def rstd_from(psum_ap, inv_n, rs, rstd, rkey_ps, key_rs, key_rstd):
        act(rs, psum_ap, AF.Sqrt, [], [rkey_ps, key_rs], scale=inv_n, bias=eps_col)
        S_.add("dve", lambda e: e.reciprocal(out=rstd, in_=rs), [key_rs], [key_rstd])

    def bc(ap2d, n):
        W = ap2d.shape[1]
        return ap2d.unsqueeze(1).to_broadcast([128, n, W])

    eps_t = cv([128, 8], F32) if cv.off + 64 <= 8192 else None
    eps_col = eps_t[:, 0:1]
    memset("pool", eps_t, EPS, [R_ALL])
    for i, (dst, src) in enumerate([(ones_sb, ones_d), (ident_sb, ident_d), (chan_sb, chan_d), (cact, cT_d),
                                    (badaT, badaT_d), (gvec, gvec_d), (convp, convp_d)]):
        dma("sp", dst, src, (), [R_ALL], "c%d" % i)
    S_.barrier()
    act(cact, cact, AF.Silu, [R_ALL], ["cact"])

    pc = Carver(BASE)
    NWS = 6
    wada = [pc([128, KC, 512], BF16) for _ in range(NWS)]
    xtok = [pc([128, 2, D], F32) for _ in range(3)]
    xgp = [pc([128, KC, G], F32) for _ in range(3)]
    cact_bf = pc([128, KC, 4], BF16)
    copy("dve", cact_bf, cact, ["cact"], ["cact_bf"])
    nblk = NMOD * D // 512
    NB = T // G

    def xpose_block(tb):
        sl = tb % 3
        dma("sp", xtok[sl], xin[tb * G:(tb + 1) * G, :].rearrange("(a p) d -> p a d", p=128),
            (), [("xtok", sl)], "xtok%d" % sl)
        for a in range(2):
            for q in range(2):
                bank = 4 + (tb * 4 + a * 2 + q) % 4
                for j in range(4):
                    kc = q * 4 + j
                    S_.add("pe", (lambda o, i: (lambda e: e.transpose(o, i, ident_sb)))(
                        ps[:, bank, j * 128:(j + 1) * 128], xtok[sl][:, a, kc * 128:(kc + 1) * 128]),
                        [("xtok", sl), R_ALL], [("ps", bank)])
                copy("act" if (a * 2 + q) % 2 == 0 else "dve",
                     xgp[sl][:, q * 4:(q + 1) * 4, a * 128:(a + 1) * 128],
                     ps[:, bank, :].rearrange("p (j t) -> p j t", j=4), [], [("ps", bank), ("xgp", sl)])
        dma("sp", xT[:, :, tb * G:(tb + 1) * G].rearrange("k p t -> p k t"), xgp[sl],
            [("xgp", sl)], [("xT", tb)], "xgp%d" % sl)

    it = 0
    tb_next = 0
    for l in range(L):
        for b in range(nblk):
            sl = it % NWS
            dma("pool", wada[sl], w_ada[l, :, b * 512:(b + 1) * 512].rearrange("(k p) n -> p k n", p=128),
                (), [("wada", sl)], "wada%d" % sl)
            for f in range(4):
                fc = b * 4 + f
                pslot = ps[:, (it * 4 + f) % 4, 0:4]
                pkey = ("ps", (it * 4 + f) % 4)
                for kc in range(KC):
                    mm(pslot, wada[sl][:, kc, f * 128:(f + 1) * 128], cact_bf[:, kc, :], kc == 0, kc == KC - 1,
                       [("wada", sl), "cact_bf"], [pkey])
                act(modT[:, l, fc, :], pslot, AF.Identity, [], [pkey, "modT"], bias=badaT[:, l, fc:fc + 1])
            it += 1
            if tb_next < NB:
                xpose_block(tb_next)
                tb_next += 1
    while tb_next < NB:
        xpose_block(tb_next)
        tb_next += 1
    for l in range(L):
        for s in range(NS):
            stt(PP[:, l, s, 0, :], modT[:, l, 8:16, s], 1.0, gvec[:, 0, l, :], ALU.add, ALU.mult, ["modT"], ["PPk"])
            tt("dve", PP[:, l, s, 1, :], modT[:, l, 16:24, s], gvec[:, 1, l, :], ALU.mult, ["modT"], ["PPk"])
            stt(PP[:, l, s, 2, :], modT[:, l, 32:40, s], 1.0, gvec[:, 2, l, :], ALU.add, ALU.mult, ["modT"], ["PPk"])
            tt("dve", PP[:, l, s, 3, :], modT[:, l, 40:48, s], gvec[:, 3, l, :], ALU.mult, ["modT"], ["PPk"])
    S_.barrier()

    mc = Carver(BASE)
    w_in_sb = mc([128, KC, 4 * DH], BF16)
    w_out_sb = mc([128, KC, D], BF16)
    f_stash = mc([128, SMAX // 128, DH], BF16)
    SW = SMAX + 8
    c_stash = mc([128, 4, SW], BF16)
    dgm = mc([128, 3, 4, 128], BF16)
    MIXP = mc.off

    def load_w(dst, src2d, rows, key, nsplit=1):
        for k in range(rows):
            dma("pool", dst[:, k, :], src2d[k * 128:(k + 1) * 128, :], (), [key], "w_%s" % key, follow=(k > 0))

    def build_diag(l):
        for tap in range(3):
            for j in range(4):
                ts("pool", dgm[:, tap, j, :], ident_sb, convp[:, l, tap, j:j + 1], None, ALU.mult, None,
                   [R_ALL], [("dgm", tap, j)])

    def mixer_p1(l, s):
        S = seqs[s]
        T0 = T0s[s]
        NG = S // G
        c1 = Carver(MIXP)
        xg = [c1([128, KC, G], F32) for _ in range(2)]
        t1b = c1([128, KC, G], F32)
        sqb = c1([128, KC, G], BF16)
        hb = [c1([128, KC, G], BF16) for _ in range(2)]
        rs_x = c1([128, G], F32)
        rstd_x = c1([128, G], F32)
        vbuf = [c1([128, G], F32) for _ in range(2)]
        ubuf = [c1([128, 4, G + 4], BF16) for _ in range(2)]
        bgbuf = [c1([128, 4, G + 4], F32) for _ in range(2)]
        co = [c1([128, 4, G], F32) for _ in range(2)]
        sqc = c1([128, 4, G], BF16)
        rs_c = c1([128, G], F32)
        rstd_c = c1([128, G], F32)
        a_m = PP[:, l, s, 0, :]

        def F1(g):
            sl = g % 2
            dma("sp", xg[sl], xT[:, :, T0 + g * G:T0 + (g + 1) * G].rearrange("k p t -> p k t"),
                [("xT", (T0 + g * G) // G)], [("xg", sl)], "xg%d" % sl)
            act(sqb, xg[sl], AF.Square, [("xg", sl)], ["sqb"])

        def F2(g):
            sl = g % 2
            for kc in range(KC):
                mm(ps[:, 6, 0:G], ones_sb, sqb[:, kc, :], kc == 0, kc == KC - 1, ["sqb", R_ALL], [("ps", 6)])
            rstd_from(ps[:, 6, 0:G], 1.0 / D, rs_x, rstd_x, ("ps", 6), "rs_x", "rstd_x")
            tt("dve", t1b, xg[sl], bc(rstd_x, KC), ALU.mult, [("xg", sl), "rstd_x"], ["t1b"])
            for kc in range(KC):
                ts("pool", hb[sl][:, kc, :], t1b[:, kc, :], a_m[:, kc:kc + 1], modT[:, l, 0 + kc, s:s + 1],
                   ALU.mult, ALU.add, ["t1b", R_ALL], [("hb", sl, kc)])

        def zgroup(g, fc, out, bank):
            sl = g % 2
            for kc in range(KC):
                mm(out, w_in_sb[:, kc, fc * 128:(fc + 1) * 128], hb[sl][:, kc, :], kc == 0, kc == KC - 1,
                   [("hb", sl, kc), "w_in"], [("ps", bank)])

        def Z(g, js):
            us = g % 2
            for j in js:
                bA = (2 * j) % 4
                bB = bA + 1
                zgroup(g, 8 + j, ps[:, bA, 0:G], bA)
                zgroup(g, 4 + j, ps[:, bA, G:2 * G], bA)
                zgroup(g, j, ps[:, bB, 0:G], bB)
                copy("act", vbuf[j % 2], ps[:, bA, 0:G], [], [("ps", bA), ("vbuf", j % 2)])
                tt("dve", ubuf[us][:, j, 2:G + 2], ps[:, bA, G:2 * G], vbuf[j % 2], ALU.mult, [("vbuf", j % 2)],
                   [("ps", bA), ("ubuf", us, j)])
                copy("act", bgbuf[us][:, j, 1:G + 1], ps[:, bB, 0:G], [], [("ps", bB), ("bgbuf", us, j)])

        def Fm(g):
            sl = g % 2
            for t_ in range(2):
                bank = (4, 7)[t_]
                for kc in range(KC):
                    mm(ps[:, bank, :], hb[sl][:, kc, t_ * 128:(t_ + 1) * 128], w_in_sb[:, kc, 3 * DH:4 * DH],
                       kc == 0, kc == KC - 1, [("hb", sl, kc), "w_in"], [("ps", bank)])
                t = g * 2 + t_
                NTH = S // 256
                if t < NTH:
                    copy("act" if t_ == 0 else "dve", f_stash[:, t, :], ps[:, bank, :], [],
                         [("ps", bank), ("fst", t)])
                else:
                    a_ = f_stash[:, t - NTH, :]
                    tt("dve", f_stash[:, t, :], a_, ps[:, bank, :], ALU.subtract, [("fst", t - NTH)],
                       [("ps", bank), ("fst", t)])
                    tt("dve", a_, a_, ps[:, bank, :], ALU.add, [], [("ps", bank), ("fst", t - NTH)])

        def H(g):
            us = g % 2
            un = (g + 1) % 2
            copy("pool", ubuf[un][:, :, 0:2], ubuf[us][:, :, G:G + 2], [("ubuf", us, j) for j in range(4)],
                 [("ubufh", un)])
            copy("pool", bgbuf[un][:, :, 0:1], bgbuf[us][:, :, G:G + 1], [("bgbuf", us, j) for j in range(4)],
                 [("bgbufh", un)])

        def C(g, W):
            us = g % 2
            for q in range(2):
                bank = 5
                for j in (2 * q, 2 * q + 1):
                    out = ps[:, bank, (j % 2) * G:(j % 2) * G + W]
                    for tap in range(3):
                        mm(out, dgm[:, tap, j, :], ubuf[us][:, j, tap:tap + W], tap == 0, tap == 2,
                           [("ubuf", us, j), ("ubufh", us), ("dgm", tap, j)], [("ps", bank)])
                for j in (2 * q, 2 * q + 1):
                    stt(co[us][:, j, 0:W], ps[:, bank, (j % 2) * G:(j % 2) * G + W], convp[:, l, 3, j:j + 1],
                        bgbuf[us][:, j, 0:W], ALU.add, ALU.mult, [("bgbuf", us, j), ("bgbufh", us), R_ALL],
                        [("ps", bank), ("co", us, j)])
            act(sqc[:, :, 0:W], co[us][:, :, 0:W], AF.Square, [("co", us, j) for j in range(4)], ["sqc"])

        def T_(g, W):
            us = g % 2
            for j in range(4):
                mm(ps[:, 6, G:G + W], ones_sb, sqc[:, j, 0:W], j == 0, j == 3, ["sqc", R_ALL], [("ps", 6)])
            rstd_from(ps[:, 6, G:G + W], 1.0 / DH, rs_c[:, 0:W], rstd_c[:, 0:W], ("ps", 6), "rs_c", "rstd_c")
            for j in range(4):
                stt(c_stash[:, j, g * G:g * G + W], co[us][:, j, 0:W], convp[:, l, 4, j:j + 1], rstd_c[:, 0:W],
                    ALU.mult, ALU.mult, [("co", us, j), "rstd_c", R_ALL], [("cst", g)])

        memset("pool", ubuf[0][:, :, 0:2], 0.0, [("ubufh", 0)])
        memset("pool", bgbuf[0][:, :, 0:1], 0.0, [("bgbufh", 0)])
        F1(0)
        F2(0)
        for g in range(NG):
            if g + 1 < NG:
                F1(g + 1)
            if g >= 1:
                C(g - 1, G)
            Z(g, (0, 1))
            if g >= 1:
                T_(g - 1, G)
            if g + 1 < NG:
                F2(g + 1)
            Z(g, (2, 3))
            Fm(g)
            H(g)
        C(NG - 1, G)
        T_(NG - 1, G)
        us = NG % 2
        memset("pool", ubuf[us][:, :, 2:4], 0.0, [("ubuf", us, j) for j in range(4)])
        memset("pool", bgbuf[us][:, :, 1:3], 0.0, [("bgbuf", us, j) for j in range(4)])
        C(NG, 2)
        T_(NG, 2)

    def mixer_p2(l, s):
        S = seqs[s]
        T0 = T0s[s]
        NSG = S // 512
        NGI = 2 * NSG
        NTH = S // 256
        NTP = min(4, NTH)
        NPC = NTH // NTP
        tab = tabs[S]
        c2 = Carver(MIXP)
        xg = [c2([128, KC, 512], F32) for _ in range(2)]
        tabp = [c2([128, NTP, 512], BF16) for _ in range(4)]
        Asb = [c2([128, 4, 512], BF16) for _ in range(2)]
        fftb = c2([128, 4, G], F32)
        sqf = c2([128, 4, G], BF16)
        fftn = [c2([128, 4, G], BF16) for _ in range(2)]
        obuf = c2([128, KC, G], F32)
        sqo = c2([128, KC, G], BF16)
        rs_f = c2([128, G], F32)
        rstd_f = c2([128, G], F32)
        rs_o = c2([128, G], F32)
        rstd_o = c2([128, G], F32)
        gg_m = PP[:, l, s, 1, :]
        wcount = [0]

        def xblk(sg):
            t0 = T0 + sg * 512
            return (xT[:, :, t0:t0 + 512].rearrange("k p t -> p k t"), [("xT", t0 // G), ("xT", t0 // G + 1)])

        def loads(gi):
            if gi % 2 == 0:
                sg = gi // 2
                ap, keys = xblk(sg)
                dma("sp", xg[sg % 2], ap, keys, [("xg", sg % 2)], "xg%d" % (sg % 2))

        pieces = [(g_, p_) for g_ in range(NGI) for p_ in range(NPC)]
        issued = [0]

        def tab_issue(upto):
            while issued[0] < min(upto, len(pieces)):
                i_ = issued[0]
                g_, p_ = pieces[i_]
                dma("pool", tabp[i_ % 4].rearrange("p a b -> p (a b)"), tab[g_, p_], (), [("tabp", i_ % 4)],
                    "tab%d" % (i_ % 4))
                issued[0] += 1

        def dft_part(gi, pcs):
            par = gi % 2
            for pc_ in pcs:
                i_ = gi * NPC + pc_
                tab_issue(i_ + 3)
                tsl = i_ % 4
                for ntl in range(NTP):
                    nt = pc_ * NTP + ntl
                    st = par * NTH + nt
                    for c in range(4):
                        mm(ps[:, c, :], f_stash[:, st, c * 128:(c + 1) * 128], tabp[tsl][:, ntl, :],
                           nt == 0, nt == NTH - 1, [("fst", st), ("tabp", tsl)], [("ps", c)])

        def a_evac(gi):
            sl = gi % 2
            for c in range(4):
                copy("act" if c % 2 == 0 else "dve", Asb[sl][:, c, :], ps[:, c, :], [], [("ps", c), ("Asb", sl, c)])

        def chan(gi):
            sl = gi % 2
            for q in range(2):
                bank = 4 + q
                for c in (2 * q, 2 * q + 1):
                    out = ps[:, bank, (c % 2) * G:(c % 2 + 1) * G]
                    mm(out, chan_sb[:, 0, :], Asb[sl][:, c, 0:G], True, False, [("Asb", sl, c), R_ALL], [("ps", bank)])
                    mm(out, chan_sb[:, 1, :], Asb[sl][:, c, G:2 * G], False, True, [("Asb", sl, c), R_ALL],
                       [("ps", bank)])
                copy("act", fftb[:, 2 * q:2 * q + 2, :], ps[:, bank, :].rearrange("p (a b) -> p a b", a=2), [],
                     [("ps", bank), ("fftb", 2 * q), ("fftb", 2 * q + 1)])
                act(sqf[:, 2 * q:2 * q + 2, :], fftb[:, 2 * q:2 * q + 2, :], AF.Square,
                    [("fftb", 2 * q), ("fftb", 2 * q + 1)], [("sqf", 2 * q), ("sqf", 2 * q + 1)])

        def ssum_f(gi):
            sl = gi % 2
            for c in range(4):
                mm(ps[:, 6, 0:G], ones_sb, sqf[:, c, :], c == 0, c == 3, [("sqf", c), R_ALL], [("ps", 6)])
            rstd_from(ps[:, 6, 0:G], 1.0 / DH, rs_f, rstd_f, ("ps", 6), "rs_f", "rstd_f")
            for c in range(4):
                stt(fftn[sl][:, c, :], fftb[:, c, :], convp[:, l, 5, c:c + 1], rstd_f, ALU.mult, ALU.mult,
                    [("fftb", c), "rstd_f", R_ALL], [("fftn", sl, c)])

        def wout(gi):
            sl = gi % 2
            sg = gi // 2
            par = gi % 2
            c0 = sg * 512 + 1 + par
            for q in range(4):
                bank = (4, 5, 7)[wcount[0] % 3]
                wcount[0] += 1
                for oc in (2 * q, 2 * q + 1):
                    out = ps[:, bank, (oc % 2) * G:(oc % 2 + 1) * G]
                    for kc in range(KC):
                        if kc < 4:
                            rhs = c_stash[:, kc, c0:c0 + 512:2]
                            rk = [("cst", 2 * sg), ("cst", 2 * sg + 1), ("cst", 2 * sg + 2)]
                        else:
                            rhs = fftn[sl][:, kc - 4, :]
                            rk = [("fftn", sl, kc - 4)]
                        mm(out, w_out_sb[:, kc, oc * 128:(oc + 1) * 128], rhs, kc == 0, kc == KC - 1,
                           rk + ["w_out"], [("ps", bank)])
                copy("act", obuf[:, 2 * q:2 * q + 2, :], ps[:, bank, :].rearrange("p (a b) -> p a b", a=2), [],
                     [("ps", bank), ("obuf", 2 * q), ("obuf", 2 * q + 1)])
                act(sqo[:, 2 * q:2 * q + 2, :], obuf[:, 2 * q:2 * q + 2, :], AF.Square,
                    [("obuf", 2 * q), ("obuf", 2 * q + 1)], [("sqo", 2 * q), ("sqo", 2 * q + 1)])

        def ssum_o(gi):
            sg = gi // 2
            par = gi % 2
            xs_ = sg % 2
            for oc in range(KC):
                mm(ps[:, 6, G:2 * G], ones_sb, sqo[:, oc, :], oc == 0, oc == KC - 1, [("sqo", oc), R_ALL],
                   [("ps", 6)])
            rstd_from(ps[:, 6, G:2 * G], 1.0 / D, rs_o, rstd_o, ("ps", 6), "rs_o", "rstd_o")
            ok = [("obuf", oc) for oc in range(KC)]
            tt("dve", obuf, obuf, bc(rstd_o, KC), ALU.mult, ok + ["rstd_o"], ok)
            for oc in range(KC):
                xv = xg[xs_][:, oc, par:512:2]
                stt(xv, obuf[:, oc, :], gg_m[:, oc:oc + 1], xv, ALU.mult, ALU.add,
                    [("obuf", oc), ("xg", xs_), R_ALL], [("xg", xs_)])
            if par == 1:
                ap, keys = xblk(sg)
                dma("sp", ap, xg[xs_], [("xg", xs_)], keys, "xgs%d" % xs_)

        nq = min(4, NPC)
        parts = [list(range(NPC))[i * NPC // nq:(i + 1) * NPC // nq] for i in range(nq)]
        while len(parts) < 4:
            parts.append([])
        for gi in range(NGI + 1):
            if gi < NGI:
                loads(gi)
                dft_part(gi, parts[0])
            if gi >= 1:
                chan(gi - 1)
            if gi < NGI:
                dft_part(gi, parts[1])
            if gi >= 1:
                ssum_f(gi - 1)
            if gi < NGI:
                dft_part(gi, parts[2])
            if gi >= 1:
                wout(gi - 1)
            if gi < NGI:
                dft_part(gi, parts[3])
                a_evac(gi)
            if gi >= 1:
                ssum_o(gi - 1)

    def ffn(l):
        fc_ = Carver(BASE)
        wg_sb = fc_([128, KC, DFF], BF16)
        wu_sb = fc_([128, KC, DFF], BF16)
        wd_sb = fc_([128, FJ, D], BF16)
        xg = [fc_([128, KC, G], F32) for _ in range(3)]
        scr_off = fc_.off
        scr = fc_([128, KC, G], F32)
        scr_y = V(scr_off, [128, 2, D], F32)
        last_layer = (l == L - 1)
        sqb = fc_([128, KC, G], BF16)
        sq2 = fc_([128, KC, G], BF16)
        hb = [fc_([128, KC, G], BF16) for _ in range(2)]
        aT = fc_([128, FJ, G], BF16)
        sgb = [fc_([128, G], BF16) for _ in range(2)]
        rs_x = fc_([128, G], F32)
        rstd_x = fc_([128, G], F32)
        rs_f = fc_([128, G], F32)
        rstd_f = fc_([128, G], F32)
        cblk = [(0, 6), (6, 12), (12, 17), (17, 22)]
        for bi, (f0, f1) in enumerate(cblk):
            for nm, dst, src in (("wg", wg_sb, w_gate), ("wu", wu_sb, w_up)):
                for k in range(KC):
                    dma("pool", dst[:, k, f0 * 128:f1 * 128], src[l, k * 128:(k + 1) * 128, f0 * 128:f1 * 128], (),
                        [(nm, bi)], "w_%s_%d" % (nm, bi), follow=(k > 0))
        fj_blk = {}
        for bi, (f0, f1) in enumerate(cblk):
            for fj_ in range(f0, f1):
                fj_blk[fj_] = bi
        load_w(wd_sb, w_down[l], FJ, "wd")
        groups = []
        for s in range(NS):
            for g in range(seqs[s] // G):
                groups.append((s, T0s[s] + g * G))
        NGt = len(groups)
        scrk = [("scr", k) for k in range(KC)]

        def xTap(i):
            t0 = groups[i][1]
            return xT[:, :, t0:t0 + G].rearrange("k p t -> p k t"), ("xT", t0 // G)

        def load(i):
            sl = i % 3
            ap, key = xTap(i)
            dma("sp", xg[sl], ap, [key], [("xg", sl)], "xg%d" % sl)
            act(sqb, xg[sl], AF.Square, [("xg", sl)], ["sqb"])

        def front2(i):
            s = groups[i][0]
            sl = i % 3
            for kc in range(KC):
                mm(ps[:, 6, 0:G], ones_sb, sqb[:, kc, :], kc == 0, kc == KC - 1, ["sqb", R_ALL], [("ps", 6)])
            rstd_from(ps[:, 6, 0:G], 1.0 / D, rs_x, rstd_x, ("ps", 6), "rs_x", "rstd_x")
            tt("dve", scr, xg[sl], bc(rstd_x, KC), ALU.mult, [("xg", sl), "rstd_x"], scrk)
            for kc in range(KC):
                act(hb[i % 2][:, kc, :], scr[:, kc, :], AF.Identity, [("scr", kc), R_ALL], [("hb", i % 2, kc)],
                    scale=PP[:, l, s, 2, kc:kc + 1], bias=modT[:, l, 24 + kc, s:s + 1])

        def gu(i, fjs):
            hs = i % 2
            for fj in fjs:
                bank = fj % 4
                for w_i, wsb in enumerate((wg_sb, wu_sb)):
                    out = ps[:, bank, w_i * G:(w_i + 1) * G]
                    for kc in range(KC):
                        mm(out, wsb[:, kc, fj * 128:(fj + 1) * 128], hb[hs][:, kc, :], kc == 0, kc == KC - 1,
                           [("hb", hs, kc), ("wg" if w_i == 0 else "wu", fj_blk[fj])], [("ps", bank)])
                sg = sgb[fj % 2]
                act(sg, ps[:, bank, 0:G], AF.Silu, [], [("ps", bank), ("sg", fj % 2)])
                tt("dve", aT[:, fj, :], ps[:, bank, G:2 * G], sg, ALU.mult, [("sg", fj % 2)], [("ps", bank), ("aT", fj)])

        def down(i):
            for q in range(4):
                bank = 4 + q % 2
                for oc in (2 * q, 2 * q + 1):
                    out = ps[:, bank, (oc % 2) * G:(oc % 2 + 1) * G]
                    for fj in range(FJ):
                        mm(out, wd_sb[:, fj, oc * 128:(oc + 1) * 128], aT[:, fj, :], fj == 0, fj == FJ - 1,
                           [("aT", fj), "wd"], [("ps", bank)])
                copy("act", scr[:, 2 * q:2 * q + 2, :], ps[:, bank, :].rearrange("p (a b) -> p a b", a=2), [],
                     [("ps", bank), ("scr", 2 * q), ("scr", 2 * q + 1)])
                act(sq2[:, 2 * q:2 * q + 2, :], scr[:, 2 * q:2 * q + 2, :], AF.Square,
                    [("scr", 2 * q), ("scr", 2 * q + 1)], [("sq2", 2 * q), ("sq2", 2 * q + 1)])

        def tail(i):
            s = groups[i][0]
            sl = i % 3
            for oc in range(KC):
                mm(ps[:, 7, 0:G], ones_sb, sq2[:, oc, :], oc == 0, oc == KC - 1, [("sq2", oc), R_ALL], [("ps", 7)])
            rstd_from(ps[:, 7, 0:G], 1.0 / D, rs_f, rstd_f, ("ps", 7), "rs_f", "rstd_f")
            tt("dve", scr, scr, bc(rstd_f, KC), ALU.mult, scrk + ["rstd_f"], scrk)
            for oc in range(KC):
                stt(xg[sl][:, oc, :], scr[:, oc, :], PP[:, l, s, 3, oc:oc + 1], xg[sl][:, oc, :], ALU.mult, ALU.add,
                    [("scr", oc), ("xg", sl), R_ALL], [("xg", sl)])
            if not last_layer:
                ap, key = xTap(i)
                dma("sp", ap, xg[sl], [("xg", sl)], [key], "xgs%d" % sl)

        def tail_b(i):
            sl = i % 3
            if last_layer:
                t0 = groups[i][1]
                for a in range(2):
                    for q in range(2):
                        bank = 4 + q
                        for j in range(4):
                            kc = q * 4 + j
                            S_.add("pe", (lambda o, i_: (lambda e: e.transpose(o, i_, ident_sb)))(
                                ps[:, bank, j * 128:(j + 1) * 128], xg[sl][:, kc, a * 128:(a + 1) * 128]),
                                [("xg", sl), R_ALL], [("ps", bank)])
                        copy("act" if q == 0 else "dve", scr_y[:, a, q * 512:(q + 1) * 512], ps[:, bank, :], [],
                             [("ps", bank)] + scrk)
                dma("sp", yout[t0:t0 + G, :].rearrange("(a p) d -> p a d", p=128), scr_y, scrk,
                    [("yout", t0 // G)], "yst")

        load(0)
        front2(0)
        for i in range(NGt):
            if i + 1 < NGt:
                load(i + 1)
            gu(i, range(0, 4))
            if i >= 1:
                tail(i - 1)
            gu(i, range(4, 12))
            if i >= 1:
                tail_b(i - 1)
            if i + 1 < NGt:
                front2(i + 1)
            gu(i, range(12, FJ))
            down(i)
        tail(NGt - 1)
        tail_b(NGt - 1)

    if do_layers:
        for l in range(L):
            load_w(w_in_sb, w_in[l], KC, "w_in")
            load_w(w_out_sb, w_out[l], KC, "w_out")
            build_diag(l)
            for s in range(NS):
                mixer_p1(l, s)
                S_.barrier()
                mixer_p2(l, s)
                S_.barrier()
            ffn(l)
            S_.barrier()

    if not do_layers:
        qc = Carver(BASE)
        xgq = [qc([128, KC, G], F32) for _ in range(2)]
        ytok = [qc([128, 2, D], F32) for _ in range(2)]
        out_keys = []
        for tb in range(NB):
            sl = tb % 2
            dma("sp", xgq[sl], xT[:, :, tb * G:(tb + 1) * G].rearrange("k p t -> p k t"),
                [("xT", tb)], [("xgq", sl)], "xgq%d" % sl)
            for a in range(2):
                for q in range(2):
                    bank = (tb * 4 + a * 2 + q) % 8
                    for j in range(4):
                        kc = q * 4 + j
                        S_.add("pe", (lambda o, i: (lambda e: e.transpose(o, i, ident_sb)))(
                            ps[:, bank, j * 128:(j + 1) * 128], xgq[sl][:, kc, a * 128:(a + 1) * 128]),
                            [("xgq", sl), R_ALL], [("ps", bank)])
                    copy("act" if (a * 2 + q) % 2 == 0 else "dve", ytok[sl][:, a, q * 512:(q + 1) * 512],
                         ps[:, bank, :], [], [("ps", bank), ("ytok", sl)])
            dma("sp", yout[tb * G:(tb + 1) * G, :].rearrange("(a p) d -> p a d", p=128), ytok[sl],
                [("ytok", sl)], [("yout", tb)], "ytok%d" % sl)


    S_.finalize()
    keys = S_.sem_keys()
    sems = {}
    for i, k in enumerate(keys):
        sems[k] = es.enter_context(nc.semaphore("s%d" % i))
    with nc.Block() as block:
        @block.tensor
        def _(e):
            S_.emit_engine("pe", e, sems)

        @block.scalar
        def _(e):
            S_.emit_engine("act", e, sems)

        @block.vector
        def _(e):
            S_.emit_engine("dve", e, sems)

        @block.gpsimd
        def _(e):
            S_.emit_engine("pool", e, sems)

        @block.sync
        def _(e):
            S_.emit_engine("sp", e, sems)
            for k, v in S_.final_counts.items():
                e.wait_ge(sems[k], v)
    es.close()
    return nc


_TAB_CACHE = {}


def _dft_table(S):
    if S in _TAB_CACHE:
        return _TAB_CACHE[S]
    assert S % 512 == 0
    H = S // 2
    NTH = H // 128
    NTP = min(4, NTH)
    NPC = NTH // NTP
    KG = S // G
    n = np.arange(H, dtype=np.int64)
    gi = np.arange(KG)
    k = (512 * (gi // 2) + (gi % 2))[:, None] + 2 * np.arange(G)[None, :]
    k = k.reshape(-1)
    m = (n[:, None] * k[None, :]) % S
    ang = m.astype(np.float64) * (2.0 * np.pi / S)
    out = np.empty((KG, NPC, 128, NTP, 2, G), dtype=ml_dtypes.bfloat16)
    for i, fn in enumerate((np.cos, np.sin)):
        M = fn(ang).astype(np.float32)
        M = M.reshape(NPC, NTP, 128, KG, G).transpose(3, 0, 2, 1, 4)
        out[:, :, :, :, i, :] = M.astype(ml_dtypes.bfloat16)
    out = np.ascontiguousarray(out.reshape(KG, NPC, 128, NTP * 512))
    _TAB_CACHE[S] = out
    return out


def _consts():
    ident = np.eye(128, dtype=np.float32)
    ones = np.ones((128, 128), dtype=ml_dtypes.bfloat16)
    c = np.arange(128)
    same = (c[:, None] // 64) == (c[None, :] // 64)
    ang = 2.0 * np.pi * ((c[:, None] % 64) * (c[None, :] % 64) % 64) / 64.0
    chan = np.zeros((128, 2, 128), dtype=np.float32)
    chan[:, 0, :] = np.where(same, np.cos(ang), 0.0)
    chan[:, 1, :] = np.where(same, -np.sin(ang), 0.0)
    return ident, ones, chan.astype(ml_dtypes.bfloat16)


def _colmajor(v, nchunk):
    v = np.asarray(v, dtype=np.float32)
    lead = v.shape[:-1]
    v = v.reshape(lead + (nchunk, 128))
    return np.ascontiguousarray(np.moveaxis(v, -1, 0))


def make_in_map(xs, cs, W, L):
    ident, ones, chan = _consts()
    NS = len(xs)
    cT = np.zeros((128, KC, 4), dtype=np.float32)
    for s in range(NS):
        cT[:, :, s] = np.asarray(cs[s], dtype=np.float32).reshape(KC, 128).T
    m = {
        "xin": np.ascontiguousarray(np.concatenate(xs, axis=0), dtype=np.float32),
        "cT": cT, "ident": ident, "ones": ones, "chan": chan,
        "badaT": _colmajor(W["b_ada"][:L], 48),
        "gvec": _colmajor(np.stack([W["g_pre_mix"][:L], W["g_post_mix"][:L], W["g_pre_ffn"][:L],
                                    W["g_post_ffn"][:L]], axis=0), KC),
        "convp": _colmajor(np.concatenate([np.asarray(W["conv_w"][:L]), np.asarray(W["conv_b"][:L])[:, None, :],
                                           np.asarray(W["g_conv"][:L])[:, None, :],
                                           np.asarray(W["g_fft"][:L])[:, None, :]], axis=1), 4),
    }
    for k in ("w_ada", "w_in", "w_out", "w_gate", "w_up", "w_down"):
        m[k] = np.ascontiguousarray(np.asarray(W[k][:L], dtype=np.float32))
    for S in sorted(set(x.shape[0] for x in xs)):
        m["tab%d" % S] = _dft_table(S)
    return m


_PROG_CACHE = {}


def kernel(x_prompt, x_sample, c_prompt, c_sample, w_ada, b_ada, g_pre_mix, g_post_mix,
           w_in, conv_w, conv_b, g_conv, g_fft, w_out, g_pre_ffn, g_post_ffn,
           w_gate, w_up, w_down):
    NCORE = 8
    L = 4
    W = dict(w_ada=w_ada, b_ada=b_ada, g_pre_mix=g_pre_mix, g_post_mix=g_post_mix, w_in=w_in,
             conv_w=conv_w, conv_b=conv_b, g_conv=g_conv, g_fft=g_fft, w_out=w_out,
             g_pre_ffn=g_pre_ffn, g_post_ffn=g_post_ffn, w_gate=w_gate, w_up=w_up, w_down=w_down)
    W = {k: np.asarray(v) for k, v in W.items()}
    x_prompt = np.asarray(x_prompt)
    x_sample = np.asarray(x_sample)
    c_prompt = np.asarray(c_prompt)
    c_sample = np.asarray(c_sample)
    Ss = x_sample.shape[1]
    Sp = x_prompt.shape[1]
    seqs = (Ss, Ss, Sp)
    key = (seqs, L)
    if key not in _PROG_CACHE:
        _PROG_CACHE[key] = build_program(list(seqs), L)
    nc = _PROG_CACHE[key]
    in_maps = []
    for c in range(NCORE):
        xs = [x_sample[2 * c], x_sample[2 * c + 1], x_prompt[c]]
        cs = [c_sample[2 * c], c_sample[2 * c + 1], c_prompt[c]]
        in_maps.append(make_in_map(xs, cs, W, L))
    res = run_bass_kernel_spmd(nc, in_maps, core_ids=list(range(NCORE)))
    y_prompt = np.empty_like(x_prompt, dtype=np.float32)
    y_sample = np.empty_like(x_sample, dtype=np.float32)
    for c in range(NCORE):
        y = res.results[c]["yout"]
        y_sample[2 * c] = y[0:Ss]
        y_sample[2 * c + 1] = y[Ss:2 * Ss]
        y_prompt[c] = y[2 * Ss:2 * Ss + Sp]
    return (y_prompt, y_sample)
```

```python
import contextlib
import numpy as np
import ml_dtypes
import concourse.bass as bass
import concourse.mybir as mybir
from concourse.bass_utils import run_bass_kernel_spmd

F32 = mybir.dt.float32
BF16 = mybir.dt.bfloat16
AF = mybir.ActivationFunctionType
ALU = mybir.AluOpType

D = 1024
KC = 8
DFF = 2816
FJ = 22
DH = 512
G = 256
NMOD = 6
EPS = 1e-6
ENGS = ("pe", "act", "dve", "pool", "sp")


class Op:
    __slots__ = ("eng", "fn", "deps", "need", "tok", "waits", "dma")


class Sched:
    def __init__(self):
        self.ops = []
        self.lastw = {}
        self.readers = {}
        self.pending = {e: [] for e in ENGS}
        self.last_by_key = {}

    def add(self, eng, fn, reads=(), writes=(), dma=None, follow=False):
        op = Op()
        op.eng = eng
        op.fn = fn
        op.dma = dma
        op.need = False
        op.tok = None
        deps = {}
        for r in reads:
            d = self.lastw.get(r)
            if d is not None:
                deps[id(d)] = d
        for w in writes:
            if follow:
                continue
            d = self.lastw.get(w)
            if d is not None:
                deps[id(d)] = d
            rd = self.readers.get(w)
            if rd:
                for d in rd.values():
                    deps[id(d)] = d
        for d in self.pending[eng]:
            deps[id(d)] = d
        self.pending[eng] = []
        key = ("dma", dma) if dma else eng
        for r in reads:
            self.readers.setdefault(r, {})[key] = op
        for w in writes:
            self.lastw[w] = op
            self.readers[w] = {}
        op.deps = list(deps.values())
        self.ops.append(op)
        self.last_by_key[key] = op
        return op

    def barrier(self):
        lst = list(self.last_by_key.values())
        for e in ENGS:
            self.pending[e] = list(lst)

    @staticmethod
    def _skip(d, op):
        return d.dma is None and op.dma is None and d.eng == "pe" and op.eng == "pe"

    def finalize(self):
        for op in self.ops:
            for d in op.deps:
                if not self._skip(d, op):
                    d.need = True
        cnt = {}
        seen = {e: {} for e in ENGS}
        for op in self.ops:
            waits = {}
            sn = seen[op.eng]
            for d in op.deps:
                if self._skip(d, op):
                    continue
                key = ("dma", d.dma) if d.dma else d.eng
                if sn.get(key, 0) >= d.tok:
                    continue
                if waits.get(key, 0) < d.tok:
                    waits[key] = d.tok
            sn.update(waits)
            op.waits = waits
            if op.dma:
                k = ("dma", op.dma)
                cnt[k] = cnt.get(k, 0) + 16
                op.tok = cnt[k]
            elif op.need:
                cnt[op.eng] = cnt.get(op.eng, 0) + 1
                op.tok = cnt[op.eng]
        self.final_counts = cnt

    def sem_keys(self):
        keys = list(ENGS)
        seen = set()
        for op in self.ops:
            if op.dma and op.dma not in seen:
                seen.add(op.dma)
                keys.append(("dma", op.dma))
        return keys

    def emit_engine(self, name, e, sems):
        for op in self.ops:
            if op.eng != name:
                continue
            for key, val in op.waits.items():
                e.wait_ge(sems[key], val)
            ins = op.fn(e)
            if op.dma:
                ins.then_inc(sems[("dma", op.dma)], 16)
            elif op.need:
                ins.then_inc(sems[op.eng], 1)


def _prod(xs):
    r = 1
    for x in xs:
        r *= x
    return r


def build_program(seqs, L, do_layers=True):
    NS = len(seqs)
    T = sum(seqs)
    T0s = [sum(seqs[:i]) for i in range(NS)]
    SMAX = max(seqs)
    nc = bass.Bass("TRN2", target_bir_lowering=False)

    def din(name, shape, dt):
        return nc.dram_tensor(name, list(shape), dt, kind="ExternalInput").ap()

    xin = din("xin", [T, D], F32)
    cT_d = din("cT", [128, KC, 4], F32)
    ident_d = din("ident", [128, 128], F32)
    ones_d = din("ones", [128, 128], BF16)
    chan_d = din("chan", [128, 2, 128], BF16)
    badaT_d = din("badaT", [128, L, 48], F32)
    gvec_d = din("gvec", [128, 4, L, KC], F32)
    convp_d = din("convp", [128, L, 6, 4], F32)
    w_ada = din("w_ada", [L, D, NMOD * D], F32)
    w_in = din("w_in", [L, D, 4 * DH], F32)
    w_out = din("w_out", [L, D, D], F32)
    w_gate = din("w_gate", [L, D, DFF], F32)
    w_up = din("w_up", [L, D, DFF], F32)
    w_down = din("w_down", [L, DFF, D], F32)
    tabs = {}
    for S in sorted(set(seqs)):
        nth = S // 256
        ntp = min(4, nth)
        tabs[S] = din("tab%d" % S, [S // G, nth // ntp, 128, ntp * 512], BF16)
    yout = nc.dram_tensor("yout", [T, D], F32, kind="ExternalOutput").ap()
    xT = nc.dram_tensor("xT", [KC, 128, T], F32).ap()

    ARENA_W = 53120
    es = contextlib.ExitStack()
    arena = es.enter_context(nc.sbuf_tensor("arena", [128, ARENA_W], F32))
    ps = es.enter_context(nc.psum_tensor("ps", [128, 8, 512], F32))

    def V(off, shape, dt):
        n = _prod(shape[1:])
        esz = 4 if dt == F32 else 2
        nb = n * esz
        assert off % 4 == 0 and nb % 4 == 0 and off + nb <= ARENA_W * 4, (off, nb)
        ap = arena[:, off // 4:(off + nb) // 4]
        if dt != F32:
            ap = ap.bitcast(dt)
        if len(shape) == 3:
            ap = ap.rearrange("p (a b) -> p a b", a=shape[1])
        elif len(shape) == 4:
            ap = ap.rearrange("p (a b c) -> p a b c", a=shape[1], b=shape[2])
        elif len(shape) == 5:
            ap = ap.rearrange("p (a b c d) -> p a b c d", a=shape[1], b=shape[2], c=shape[3])
        return ap

    class Carver:
        def __init__(self, base):
            self.off = base

        def __call__(self, shape, dt):
            n = _prod(shape[1:]) * (4 if dt == F32 else 2)
            n = (n + 31) // 32 * 32
            v = V(self.off, shape, dt)
            self.off += n
            return v

    cv = Carver(0)
    ones_sb = cv([128, 128], BF16)
    ident_sb = cv([128, 128], F32)
    chan_sb = cv([128, 2, 128], BF16)
    cact = cv([128, KC, 4], F32)
    modT = cv([128, L, 48, 4], F32)
    badaT = cv([128, L, 48], F32)
    gvec = cv([128, 4, L, KC], F32)
    convp = cv([128, L, 6, 4], F32)
    PP = cv([128, L, NS, 4, KC], F32)
    CONST_END = (cv.off + 255) // 256 * 256
    assert CONST_END <= 8192, CONST_END
    BASE = 8192

    S_ = Sched()
    R_ALL = "const"

    def dma(q, out, in_, reads, writes, sem, follow=False):
        S_.add(q, lambda e: e.dma_start(out=out, in_=in_), reads, writes, dma=sem, follow=follow)

    def mm(out, lhsT, rhs, start, stop, reads, writes):
        S_.add("pe", lambda e: e.matmul(out, lhsT=lhsT, rhs=rhs, start=start, stop=stop), reads, writes)

    def act(out, in_, func, reads, writes, scale=None, bias=None):
        kw = {}
        if scale is not None:
            kw["scale"] = scale
        if bias is not None:
            kw["bias"] = bias
        S_.add("act", lambda e: e.activation(out=out, in_=in_, func=func, **kw), reads, writes)

    def copy(eng, out, in_, reads, writes):
        if eng == "act":
            act(out, in_, AF.Copy, reads, writes)
        else:
            S_.add(eng, lambda e: e.tensor_copy(out=out, in_=in_), reads, writes)

    def tt(eng, out, in0, in1, op, reads, writes):
        S_.add(eng, lambda e: e.tensor_tensor(out=out, in0=in0, in1=in1, op=op), reads, writes)

    def stt(out, in0, scalar, in1, op0, op1, reads, writes):
        S_.add("dve", lambda e: e.scalar_tensor_tensor(out=out, in0=in0, scalar=scalar, in1=in1, op0=op0, op1=op1),
               reads, writes)

    def ts(eng, out, in0, s1, s2, op0, op1, reads, writes):
        if s2 is None:
            S_.add(eng, lambda e: e.tensor_scalar(out=out, in0=in0, scalar1=s1, scalar2=None, op0=op0), reads, writes)
        else:
            S_.add(eng, lambda e: e.tensor_scalar(out=out, in0=in0, scalar1=s1, scalar2=s2, op0=op0, op1=op1),
                   reads, writes)

    def memset(eng, out, val, writes):
        S_.add(eng, lambda e: e.memset(out, val), (), writes)

    def rstd_from(psum_ap, inv_n, rs, rstd, rkey_ps, key_rs, key_rstd):
        act(rs, psum_ap, AF.Sqrt, [], [rkey_ps, key_rs], scale=inv_n, bias=eps_col)
        S_.add("dve", lambda e: e.reciprocal(out=rstd, in_=rs), [key_rs], [key_rstd])

    def tree_sum(eng, buf, n, W, keys_in, key_out):
        m = n
        while m > 1:
            h_ = m // 2
            tt(eng, buf[:, 0:h_, 0:W], buf[:, 0:h_, 0:W], buf[:, h_:m, 0:W], ALU.add, keys_in, keys_in + [key_out])
            m = h_

    def bc(ap2d, n):
        W = ap2d.shape[1]
        return ap2d.unsqueeze(1).to_broadcast([128, n, W])

    eps_t = cv([128, 8], F32) if cv.off + 64 <= 8192 else None
    eps_col = eps_t[:, 0:1]
    memset("pool", eps_t, EPS, [R_ALL])
    for i, (dst, src) in enumerate([(ones_sb, ones_d), (ident_sb, ident_d), (chan_sb, chan_d), (cact, cT_d),
                                    (badaT, badaT_d), (gvec, gvec_d), (convp, convp_d)]):
        dma("sp", dst, src, (), [R_ALL], "c%d" % i)
    S_.barrier()
    act(cact, cact, AF.Silu, [R_ALL], ["cact"])

    pc = Carver(BASE)
    NWS = 8
    wada = [pc([128, KC, 512], BF16) for _ in range(NWS)]
    xtok = [pc([128, 2, D], F32) for _ in range(4)]
    xgp = [pc([128, KC, G], F32) for _ in range(4)]
    cact_bf = pc([128, KC, 4], BF16)
    copy("dve", cact_bf, cact, ["cact"], ["cact_bf"])
    nblk = NMOD * D // 512
    NB = T // G

    def xpose_block(tb):
        sl = tb % 4
        dma("sp", xtok[sl], xin[tb * G:(tb + 1) * G, :].rearrange("(a p) d -> p a d", p=128),
            (), [("xtok", sl)], "xtok%d" % sl)
        for a in range(2):
            for q in range(2):
                bank = 4 + (tb * 4 + a * 2 + q) % 4
                for j in range(4):
                    kc = q * 4 + j
                    S_.add("pe", (lambda o, i: (lambda e: e.transpose(o, i, ident_sb)))(
                        ps[:, bank, j * 128:(j + 1) * 128], xtok[sl][:, a, kc * 128:(kc + 1) * 128]),
                        [("xtok", sl), R_ALL], [("ps", bank)])
                copy("act" if (a * 2 + q) % 2 == 0 else "dve",
                     xgp[sl][:, q * 4:(q + 1) * 4, a * 128:(a + 1) * 128],
                     ps[:, bank, :].rearrange("p (j t) -> p j t", j=4), [], [("ps", bank), ("xgp", sl)])
        dma("sp", xT[:, :, tb * G:(tb + 1) * G].rearrange("k p t -> p k t"), xgp[sl],
            [("xgp", sl)], [("xT", tb)], "xgp%d" % sl)

    it = 0
    tb_next = 0
    for l in range(L):
        for b in range(nblk):
            sl = it % NWS
            dma("pool", wada[sl], w_ada[l, :, b * 512:(b + 1) * 512].rearrange("(k p) n -> p k n", p=128),
                (), [("wada", sl)], "wada%d" % sl)
            for f in range(4):
                fc = b * 4 + f
                pslot = ps[:, (it * 4 + f) % 4, 0:4]
                pkey = ("ps", (it * 4 + f) % 4)
                for kc in range(KC):
                    mm(pslot, wada[sl][:, kc, f * 128:(f + 1) * 128], cact_bf[:, kc, :], kc == 0, kc == KC - 1,
                       [("wada", sl), "cact_bf"], [pkey])
                act(modT[:, l, fc, :], pslot, AF.Identity, [], [pkey, "modT"], bias=badaT[:, l, fc:fc + 1])
            it += 1
            if tb_next < NB:
                xpose_block(tb_next)
                tb_next += 1
    while tb_next < NB:
        xpose_block(tb_next)
        tb_next += 1
    for l in range(L):
        for s in range(NS):
            stt(PP[:, l, s, 0, :], modT[:, l, 8:16, s], 1.0, gvec[:, 0, l, :], ALU.add, ALU.mult, ["modT"], ["PPk"])
            tt("dve", PP[:, l, s, 1, :], modT[:, l, 16:24, s], gvec[:, 1, l, :], ALU.mult, ["modT"], ["PPk"])
            stt(PP[:, l, s, 2, :], modT[:, l, 32:40, s], 1.0, gvec[:, 2, l, :], ALU.add, ALU.mult, ["modT"], ["PPk"])
            tt("dve", PP[:, l, s, 3, :], modT[:, l, 40:48, s], gvec[:, 3, l, :], ALU.mult, ["modT"], ["PPk"])
    S_.barrier()

    mc = Carver(BASE)
    w_in_sb = mc([128, KC, 4 * DH], BF16)
    w_out_sb = mc([128, KC, D], BF16)
    f_stash = mc([128, SMAX // 128, DH], BF16)
    SW = SMAX + 8
    c_stash = mc([128, 4, SW], BF16)
    dgm = mc([128, 3, 4, 128], BF16)
    MIXP = mc.off

    def load_w(dst, src2d, rows, key, nsplit=1):
        for k in range(rows):
            dma("pool", dst[:, k, :], src2d[k * 128:(k + 1) * 128, :], (), [key], "w_%s" % key, follow=(k > 0))

    def build_diag(l):
        for tap in range(3):
            for j in range(4):
                ts("pool", dgm[:, tap, j, :], ident_sb, convp[:, l, tap, j:j + 1], None, ALU.mult, None,
                   [R_ALL], [("dgm", tap, j)])

    def mixer_p1(l, s):
        S = seqs[s]
        T0 = T0s[s]
        NG = S // G
        c1 = Carver(MIXP)
        xg = [c1([128, KC, G], F32) for _ in range(2)]
        t1b = c1([128, KC, G], F32)
        sqb = c1([128, KC, G], BF16)
        hb = [c1([128, KC, G], BF16) for _ in range(2)]
        rs_x = c1([128, G], F32)
        rstd_x = c1([128, G], F32)
        vbuf = [c1([128, G], F32) for _ in range(2)]
        ubuf = [c1([128, 4, G + 4], BF16) for _ in range(2)]
        bgbuf = [c1([128, 4, G + 4], F32) for _ in range(2)]
        co = [c1([128, 4, G], F32) for _ in range(2)]
        sqc = c1([128, 4, G], BF16)
        rs_c = c1([128, G], F32)
        rstd_c = c1([128, G], F32)
        a_m = PP[:, l, s, 0, :]

        def F1(g):
            sl = g % 2
            dma("sp", xg[sl], xT[:, :, T0 + g * G:T0 + (g + 1) * G].rearrange("k p t -> p k t"),
                [("xT", (T0 + g * G) // G)], [("xg", sl)], "xg%d" % sl)
            act(sqb, xg[sl], AF.Square, [("xg", sl)], ["sqb"])

        def F2(g):
            sl = g % 2
            for kc in range(KC):
                mm(ps[:, 6, 0:G], ones_sb, sqb[:, kc, :], kc == 0, kc == KC - 1, ["sqb", R_ALL], [("ps", 6)])
            rstd_from(ps[:, 6, 0:G], 1.0 / D, rs_x, rstd_x, ("ps", 6), "rs_x", "rstd_x")
            tt("dve", t1b, xg[sl], bc(rstd_x, KC), ALU.mult, [("xg", sl), "rstd_x"], ["t1b"])
            for kc in range(KC):
                ts("pool", hb[sl][:, kc, :], t1b[:, kc, :], a_m[:, kc:kc + 1], modT[:, l, 0 + kc, s:s + 1],
                   ALU.mult, ALU.add, ["t1b", R_ALL], [("hb", sl, kc)])

        def zgroup(g, fc, out, bank):
            sl = g % 2
            for kc in range(KC):
                mm(out, w_in_sb[:, kc, fc * 128:(fc + 1) * 128], hb[sl][:, kc, :], kc == 0, kc == KC - 1,
                   [("hb", sl, kc), "w_in"], [("ps", bank)])

        def Z(g, js):
            us = g % 2
            for j in js:
                bA = (2 * j) % 4
                bB = bA + 1
                zgroup(g, 8 + j, ps[:, bA, 0:G], bA)
                zgroup(g, 4 + j, ps[:, bA, G:2 * G], bA)
                zgroup(g, j, ps[:, bB, 0:G], bB)
                copy("act", vbuf[j % 2], ps[:, bA, 0:G], [], [("ps", bA), ("vbuf", j % 2)])
                tt("dve", ubuf[us][:, j, 2:G + 2], ps[:, bA, G:2 * G], vbuf[j % 2], ALU.mult, [("vbuf", j % 2)],
                   [("ps", bA), ("ubuf", us, j)])
                copy("act", bgbuf[us][:, j, 1:G + 1], ps[:, bB, 0:G], [], [("ps", bB), ("bgbuf", us, j)])

        def Fm(g):
            sl = g % 2
            for t_ in range(2):
                bank = (4, 7)[t_]
                for kc in range(KC):
                    mm(ps[:, bank, :], hb[sl][:, kc, t_ * 128:(t_ + 1) * 128], w_in_sb[:, kc, 3 * DH:4 * DH],
                       kc == 0, kc == KC - 1, [("hb", sl, kc), "w_in"], [("ps", bank)])
                t = g * 2 + t_
                NTH = S // 256
                if t < NTH:
                    copy("act" if t_ == 0 else "dve", f_stash[:, t, :], ps[:, bank, :], [],
                         [("ps", bank), ("fst", t)])
                else:
                    a_ = f_stash[:, t - NTH, :]
                    tt("dve", f_stash[:, t, :], a_, ps[:, bank, :], ALU.subtract, [("fst", t - NTH)],
                       [("ps", bank), ("fst", t)])
                    tt("dve", a_, a_, ps[:, bank, :], ALU.add, [], [("ps", bank), ("fst", t - NTH)])

        def H(g):
            us = g % 2
            un = (g + 1) % 2
            copy("pool", ubuf[un][:, :, 0:2], ubuf[us][:, :, G:G + 2], [("ubuf", us, j) for j in range(4)],
                 [("ubufh", un)])
            copy("pool", bgbuf[un][:, :, 0:1], bgbuf[us][:, :, G:G + 1], [("bgbuf", us, j) for j in range(4)],
                 [("bgbufh", un)])

        def C(g, W):
            us = g % 2
            for q in range(2):
                bank = 5
                for j in (2 * q, 2 * q + 1):
                    out = ps[:, bank, (j % 2) * G:(j % 2) * G + W]
                    for tap in range(3):
                        mm(out, dgm[:, tap, j, :], ubuf[us][:, j, tap:tap + W], tap == 0, tap == 2,
                           [("ubuf", us, j), ("ubufh", us), ("dgm", tap, j)], [("ps", bank)])
                for j in (2 * q, 2 * q + 1):
                    stt(co[us][:, j, 0:W], ps[:, bank, (j % 2) * G:(j % 2) * G + W], convp[:, l, 3, j:j + 1],
                        bgbuf[us][:, j, 0:W], ALU.add, ALU.mult, [("bgbuf", us, j), ("bgbufh", us), R_ALL],
                        [("ps", bank), ("co", us, j)])
            act(sqc[:, :, 0:W], co[us][:, :, 0:W], AF.Square, [("co", us, j) for j in range(4)], ["sqc"])

        def T_(g, W):
            us = g % 2
            for j in range(4):
                mm(ps[:, 6, G:G + W], ones_sb, sqc[:, j, 0:W], j == 0, j == 3, ["sqc", R_ALL], [("ps", 6)])
            rstd_from(ps[:, 6, G:G + W], 1.0 / DH, rs_c[:, 0:W], rstd_c[:, 0:W], ("ps", 6), "rs_c", "rstd_c")
            for j in range(4):
                stt(c_stash[:, j, g * G:g * G + W], co[us][:, j, 0:W], convp[:, l, 4, j:j + 1], rstd_c[:, 0:W],
                    ALU.mult, ALU.mult, [("co", us, j), "rstd_c", R_ALL], [("cst", g)])

        memset("pool", ubuf[0][:, :, 0:2], 0.0, [("ubufh", 0)])
        memset("pool", bgbuf[0][:, :, 0:1], 0.0, [("bgbufh", 0)])
        F1(0)
        F2(0)
        for g in range(NG):
            if g + 1 < NG:
                F1(g + 1)
            if g >= 1:
                C(g - 1, G)
            Z(g, (0, 1))
            if g >= 1:
                T_(g - 1, G)
            if g + 1 < NG:
                F2(g + 1)
            Z(g, (2, 3))
            Fm(g)
            H(g)
        C(NG - 1, G)
        T_(NG - 1, G)
        us = NG % 2
        memset("pool", ubuf[us][:, :, 2:4], 0.0, [("ubuf", us, j) for j in range(4)])
        memset("pool", bgbuf[us][:, :, 1:3], 0.0, [("bgbuf", us, j) for j in range(4)])
        C(NG, 2)
        T_(NG, 2)

    def mixer_p2(l, s):
        S = seqs[s]
        T0 = T0s[s]
        NSG = S // 512
        NGI = 2 * NSG
        NTH = S // 256
        NTP = min(4, NTH)
        NPC = NTH // NTP
        tab = tabs[S]
        c2 = Carver(MIXP)
        xg = [c2([128, KC, 512], F32) for _ in range(2)]
        tabp = [c2([128, NTP, 512], BF16) for _ in range(4)]
        Asb = [c2([128, 4, 512], BF16) for _ in range(2)]
        fftb = c2([128, 4, G], F32)
        sqf = c2([128, 4, G], BF16)
        fftn = [c2([128, 4, G], BF16) for _ in range(2)]
        obuf = c2([128, KC, G], F32)
        sqo = c2([128, KC, G], BF16)
        rs_f = c2([128, G], F32)
        rstd_f = c2([128, G], F32)
        rs_o = c2([128, G], F32)
        rstd_o = c2([128, G], F32)
        gg_m = PP[:, l, s, 1, :]
        wcount = [0]

        def xblk(sg):
            t0 = T0 + sg * 512
            return (xT[:, :, t0:t0 + 512].rearrange("k p t -> p k t"), [("xT", t0 // G), ("xT", t0 // G + 1)])

        def loads(gi):
            if gi % 2 == 0:
                sg = gi // 2
                ap, keys = xblk(sg)
                dma("sp", xg[sg % 2], ap, keys, [("xg", sg % 2)], "xg%d" % (sg % 2))

        pieces = [(g_, p_) for g_ in range(NGI) for p_ in range(NPC)]
        issued = [0]

        def tab_issue(upto):
            while issued[0] < min(upto, len(pieces)):
                i_ = issued[0]
                g_, p_ = pieces[i_]
                dma("pool", tabp[i_ % 4].rearrange("p a b -> p (a b)"), tab[g_, p_], (), [("tabp", i_ % 4)],
                    "tab%d" % (i_ % 4))
                issued[0] += 1

        def dft_part(gi, pcs):
            par = gi % 2
            for pc_ in pcs:
                i_ = gi * NPC + pc_
                tab_issue(i_ + 3)
                tsl = i_ % 4
                for ntl in range(NTP):
                    nt = pc_ * NTP + ntl
                    st = par * NTH + nt
                    for c in range(4):
                        mm(ps[:, c, :], f_stash[:, st, c * 128:(c + 1) * 128], tabp[tsl][:, ntl, :],
                           nt == 0, nt == NTH - 1, [("fst", st), ("tabp", tsl)], [("ps", c)])

        def a_evac(gi):
            sl = gi % 2
            for c in range(4):
                copy("act" if c % 2 == 0 else "dve", Asb[sl][:, c, :], ps[:, c, :], [], [("ps", c), ("Asb", sl, c)])

        def chan(gi):
            sl = gi % 2
            for q in range(2):
                bank = 4 + q
                for c in (2 * q, 2 * q + 1):
                    out = ps[:, bank, (c % 2) * G:(c % 2 + 1) * G]
                    mm(out, chan_sb[:, 0, :], Asb[sl][:, c, 0:G], True, False, [("Asb", sl, c), R_ALL], [("ps", bank)])
                    mm(out, chan_sb[:, 1, :], Asb[sl][:, c, G:2 * G], False, True, [("Asb", sl, c), R_ALL],
                       [("ps", bank)])
                copy("act", fftb[:, 2 * q:2 * q + 2, :], ps[:, bank, :].rearrange("p (a b) -> p a b", a=2), [],
                     [("ps", bank), ("fftb", 2 * q), ("fftb", 2 * q + 1)])
                act(sqf[:, 2 * q:2 * q + 2, :], fftb[:, 2 * q:2 * q + 2, :], AF.Square,
                    [("fftb", 2 * q), ("fftb", 2 * q + 1)], [("sqf", 2 * q), ("sqf", 2 * q + 1)])

        def ssum_f(gi):
            sl = gi % 2
            for c in range(4):
                mm(ps[:, 6, 0:G], ones_sb, sqf[:, c, :], c == 0, c == 3, [("sqf", c), R_ALL], [("ps", 6)])
            rstd_from(ps[:, 6, 0:G], 1.0 / DH, rs_f, rstd_f, ("ps", 6), "rs_f", "rstd_f")
            for c in range(4):
                stt(fftn[sl][:, c, :], fftb[:, c, :], convp[:, l, 5, c:c + 1], rstd_f, ALU.mult, ALU.mult,
                    [("fftb", c), "rstd_f", R_ALL], [("fftn", sl, c)])

        def wout(gi):
            sl = gi % 2
            sg = gi // 2
            par = gi % 2
            c0 = sg * 512 + 1 + par
            for q in range(4):
                bank = (4, 5, 7)[wcount[0] % 3]
                wcount[0] += 1
                for oc in (2 * q, 2 * q + 1):
                    out = ps[:, bank, (oc % 2) * G:(oc % 2 + 1) * G]
                    for kc in range(KC):
                        if kc < 4:
                            rhs = c_stash[:, kc, c0:c0 + 512:2]
                            rk = [("cst", 2 * sg), ("cst", 2 * sg + 1), ("cst", 2 * sg + 2)]
                        else:
                            rhs = fftn[sl][:, kc - 4, :]
                            rk = [("fftn", sl, kc - 4)]
                        mm(out, w_out_sb[:, kc, oc * 128:(oc + 1) * 128], rhs, kc == 0, kc == KC - 1,
                           rk + ["w_out"], [("ps", bank)])
                copy("act", obuf[:, 2 * q:2 * q + 2, :], ps[:, bank, :].rearrange("p (a b) -> p a b", a=2), [],
                     [("ps", bank), ("obuf", 2 * q), ("obuf", 2 * q + 1)])
                act(sqo[:, 2 * q:2 * q + 2, :], obuf[:, 2 * q:2 * q + 2, :], AF.Square,
                    [("obuf", 2 * q), ("obuf", 2 * q + 1)], [("sqo", 2 * q), ("sqo", 2 * q + 1)])

        def ssum_o(gi):
            sg = gi // 2
            par = gi % 2
            xs_ = sg % 2
            for oc in range(KC):
                mm(ps[:, 6, G:2 * G], ones_sb, sqo[:, oc, :], oc == 0, oc == KC - 1, [("sqo", oc), R_ALL],
                   [("ps", 6)])
            rstd_from(ps[:, 6, G:2 * G], 1.0 / D, rs_o, rstd_o, ("ps", 6), "rs_o", "rstd_o")
            ok = [("obuf", oc) for oc in range(KC)]
            tt("dve", obuf, obuf, bc(rstd_o, KC), ALU.mult, ok + ["rstd_o"], ok)
            for oc in range(KC):
                xv = xg[xs_][:, oc, par:512:2]
                stt(xv, obuf[:, oc, :], gg_m[:, oc:oc + 1], xv, ALU.mult, ALU.add,
                    [("obuf", oc), ("xg", xs_), R_ALL], [("xg", xs_)])
            if par == 1:
                ap, keys = xblk(sg)
                dma("sp", ap, xg[xs_], [("xg", xs_)], keys, "xgs%d" % xs_)

        nq = min(4, NPC)
        parts = [list(range(NPC))[i * NPC // nq:(i + 1) * NPC // nq] for i in range(nq)]
        while len(parts) < 4:
            parts.append([])
        for gi in range(NGI + 1):
            if gi < NGI:
                loads(gi)
                dft_part(gi, parts[0])
            if gi >= 1:
                chan(gi - 1)
            if gi < NGI:
                dft_part(gi, parts[1])
            if gi >= 1:
                ssum_f(gi - 1)
            if gi < NGI:
                dft_part(gi, parts[2])
            if gi >= 1:
                wout(gi - 1)
            if gi < NGI:
                dft_part(gi, parts[3])
                a_evac(gi)
            if gi >= 1:
                ssum_o(gi - 1)

    def ffn(l):
        fc_ = Carver(BASE)
        wg_sb = fc_([128, KC, DFF], BF16)
        wu_sb = fc_([128, KC, DFF], BF16)
        wd_sb = fc_([128, FJ, D], BF16)
        xg = [fc_([128, KC, G], F32) for _ in range(3)]
        scr_off = fc_.off
        scr = fc_([128, KC, G], F32)
        scr_y = V(scr_off, [128, 2, D], F32)
        last_layer = (l == L - 1)
        sqb = fc_([128, KC, G], BF16)
        sq2 = fc_([128, KC, G], BF16)
        hb = [fc_([128, KC, G], BF16) for _ in range(2)]
        aT = fc_([128, FJ, G], BF16)
        sgb = [fc_([128, G], BF16) for _ in range(2)]
        rs_x = fc_([128, G], F32)
        rstd_x = fc_([128, G], F32)
        rs_f = fc_([128, G], F32)
        rstd_f = fc_([128, G], F32)
        cblk = [(0, 6), (6, 12), (12, 17), (17, 22)]
        for bi, (f0, f1) in enumerate(cblk):
            for nm, dst, src in (("wg", wg_sb, w_gate), ("wu", wu_sb, w_up)):
                for k in range(KC):
                    dma("pool", dst[:, k, f0 * 128:f1 * 128], src[l, k * 128:(k + 1) * 128, f0 * 128:f1 * 128], (),
                        [(nm, bi)], "w_%s_%d" % (nm, bi), follow=(k > 0))
        fj_blk = {}
        for bi, (f0, f1) in enumerate(cblk):
            for fj_ in range(f0, f1):
                fj_blk[fj_] = bi
        load_w(wd_sb, w_down[l], FJ, "wd")
        groups = []
        for s in range(NS):
            for g in range(seqs[s] // G):
                groups.append((s, T0s[s] + g * G))
        NGt = len(groups)
        scrk = [("scr", k) for k in range(KC)]

        def xTap(i):
            t0 = groups[i][1]
            return xT[:, :, t0:t0 + G].rearrange("k p t -> p k t"), ("xT", t0 // G)

        def load(i):
            sl = i % 3
            ap, key = xTap(i)
            dma("sp", xg[sl], ap, [key], [("xg", sl)], "xg%d" % sl)
            act(sqb, xg[sl], AF.Square, [("xg", sl)], ["sqb"])

        def front2(i):
            s = groups[i][0]
            sl = i % 3
            tree_sum("pool", sqb, KC, G, ["sqb"], "sqbs")
            mm(ps[:, 6, 0:G], ones_sb, sqb[:, 0, :], True, True, ["sqb", "sqbs", R_ALL], [("ps", 6)])
            rstd_from(ps[:, 6, 0:G], 1.0 / D, rs_x, rstd_x, ("ps", 6), "rs_x", "rstd_x")
            tt("dve", scr, xg[sl], bc(rstd_x, KC), ALU.mult, [("xg", sl), "rstd_x"], scrk)
            for kc in range(KC):
                act(hb[i % 2][:, kc, :], scr[:, kc, :], AF.Identity, [("scr", kc), R_ALL], [("hb", i % 2, kc)],
                    scale=PP[:, l, s, 2, kc:kc + 1], bias=modT[:, l, 24 + kc, s:s + 1])

        def gu(i, fjs):
            hs = i % 2
            for fj in fjs:
                bank = fj % 4
                for w_i, wsb in enumerate((wg_sb, wu_sb)):
                    out = ps[:, bank, w_i * G:(w_i + 1) * G]
                    for kc in range(KC):
                        mm(out, wsb[:, kc, fj * 128:(fj + 1) * 128], hb[hs][:, kc, :], kc == 0, kc == KC - 1,
                           [("hb", hs, kc), ("wg" if w_i == 0 else "wu", fj_blk[fj])], [("ps", bank)])
                sg = sgb[fj % 2]
                act(sg, ps[:, bank, 0:G], AF.Silu, [], [("ps", bank), ("sg", fj % 2)])
                tt("dve", aT[:, fj, :], ps[:, bank, G:2 * G], sg, ALU.mult, [("sg", fj % 2)], [("ps", bank), ("aT", fj)])

        def down(i):
            for q in range(4):
                bank = 4 + q % 2
                for oc in (2 * q, 2 * q + 1):
                    out = ps[:, bank, (oc % 2) * G:(oc % 2 + 1) * G]
                    for fj in range(FJ):
                        mm(out, wd_sb[:, fj, oc * 128:(oc + 1) * 128], aT[:, fj, :], fj == 0, fj == FJ - 1,
                           [("aT", fj), "wd"], [("ps", bank)])
                copy("act", scr[:, 2 * q:2 * q + 2, :], ps[:, bank, :].rearrange("p (a b) -> p a b", a=2), [],
                     [("ps", bank), ("scr", 2 * q), ("scr", 2 * q + 1)])
                act(sq2[:, 2 * q:2 * q + 2, :], scr[:, 2 * q:2 * q + 2, :], AF.Square,
                    [("scr", 2 * q), ("scr", 2 * q + 1)], [("sq2", 2 * q), ("sq2", 2 * q + 1)])

        def tail(i):
            s = groups[i][0]
            sl = i % 3
            sq2k = [("sq2", oc) for oc in range(KC)]
            tree_sum("pool", sq2, KC, G, sq2k, "sq2s")
            mm(ps[:, 7, 0:G], ones_sb, sq2[:, 0, :], True, True, sq2k + ["sq2s", R_ALL], [("ps", 7)])
            rstd_from(ps[:, 7, 0:G], 1.0 / D, rs_f, rstd_f, ("ps", 7), "rs_f", "rstd_f")
            tt("dve", scr, scr, bc(rstd_f, KC), ALU.mult, scrk + ["rstd_f"], scrk)
            for oc in range(KC):
                stt(xg[sl][:, oc, :], scr[:, oc, :], PP[:, l, s, 3, oc:oc + 1], xg[sl][:, oc, :], ALU.mult, ALU.add,
                    [("scr", oc), ("xg", sl), R_ALL], [("xg", sl)])
            if not last_layer:
                ap, key = xTap(i)
                dma("sp", ap, xg[sl], [("xg", sl)], [key], "xgs%d" % sl)

        def tail_b(i):
            sl = i % 3
            if last_layer:
                t0 = groups[i][1]
                for a in range(2):
                    for q in range(2):
                        bank = 4 + q
                        for j in range(4):
                            kc = q * 4 + j
                            S_.add("pe", (lambda o, i_: (lambda e: e.transpose(o, i_, ident_sb)))(
                                ps[:, bank, j * 128:(j + 1) * 128], xg[sl][:, kc, a * 128:(a + 1) * 128]),
                                [("xg", sl), R_ALL], [("ps", bank)])
                        copy("act" if q == 0 else "dve", scr_y[:, a, q * 512:(q + 1) * 512], ps[:, bank, :], [],
                             [("ps", bank)] + scrk)
                dma("sp", yout[t0:t0 + G, :].rearrange("(a p) d -> p a d", p=128), scr_y, scrk,
                    [("yout", t0 // G)], "yst")

        load(0)
        front2(0)
        for i in range(NGt):
            if i + 1 < NGt:
                load(i + 1)
            gu(i, range(0, 4))
            if i >= 1:
                tail(i - 1)
            gu(i, range(4, 12))
            if i >= 1:
                tail_b(i - 1)
            if i + 1 < NGt:
                front2(i + 1)
            gu(i, range(12, FJ))
            down(i)
        tail(NGt - 1)
        tail_b(NGt - 1)

    if do_layers:
        for l in range(L):
            load_w(w_in_sb, w_in[l], KC, "w_in")
            load_w(w_out_sb, w_out[l], KC, "w_out")
            build_diag(l)
            for s in range(NS):
                mixer_p1(l, s)
                S_.barrier()
                mixer_p2(l, s)
                S_.barrier()
            ffn(l)
            S_.barrier()

    if not do_layers:
        qc = Carver(BASE)
        xgq = [qc([128, KC, G], F32) for _ in range(2)]
        ytok = [qc([128, 2, D], F32) for _ in range(2)]
        out_keys = []
        for tb in range(NB):
            sl = tb % 2
            dma("sp", xgq[sl], xT[:, :, tb * G:(tb + 1) * G].rearrange("k p t -> p k t"),
                [("xT", tb)], [("xgq", sl)], "xgq%d" % sl)
            for a in range(2):
                for q in range(2):
                    bank = (tb * 4 + a * 2 + q) % 8
                    for j in range(4):
                        kc = q * 4 + j
                        S_.add("pe", (lambda o, i: (lambda e: e.transpose(o, i, ident_sb)))(
                            ps[:, bank, j * 128:(j + 1) * 128], xgq[sl][:, kc, a * 128:(a + 1) * 128]),
                            [("xgq", sl), R_ALL], [("ps", bank)])
                    copy("act" if (a * 2 + q) % 2 == 0 else "dve", ytok[sl][:, a, q * 512:(q + 1) * 512],
                         ps[:, bank, :], [], [("ps", bank), ("ytok", sl)])
            dma("sp", yout[tb * G:(tb + 1) * G, :].rearrange("(a p) d -> p a d", p=128), ytok[sl],
                [("ytok", sl)], [("yout", tb)], "ytok%d" % sl)


    S_.finalize()
    keys = S_.sem_keys()
    sems = {}
    for i, k in enumerate(keys):
        sems[k] = es.enter_context(nc.semaphore("s%d" % i))
    with nc.Block() as block:
        @block.tensor
        def _(e):
            S_.emit_engine("pe", e, sems)

        @block.scalar
        def _(e):
            S_.emit_engine("act", e, sems)

        @block.vector
        def _(e):
            S_.emit_engine("dve", e, sems)

        @block.gpsimd
        def _(e):
            S_.emit_engine("pool", e, sems)

        @block.sync
        def _(e):
            S_.emit_engine("sp", e, sems)
            for k, v in S_.final_counts.items():
                e.wait_ge(sems[k], v)
    es.close()
    return nc


_TAB_CACHE = {}


def _dft_table(S):
    if S in _TAB_CACHE:
        return _TAB_CACHE[S]
    assert S % 512 == 0
    H = S // 2
    NTH = H // 128
    NTP = min(4, NTH)
    NPC = NTH // NTP
    KG = S // G
    n = np.arange(H, dtype=np.int64)
    gi = np.arange(KG)
    k = (512 * (gi // 2) + (gi % 2))[:, None] + 2 * np.arange(G)[None, :]
    k = k.reshape(-1)
    m = (n[:, None] * k[None, :]) % S
    ang = m.astype(np.float64) * (2.0 * np.pi / S)
    out = np.empty((KG, NPC, 128, NTP, 2, G), dtype=ml_dtypes.bfloat16)
    for i, fn in enumerate((np.cos, np.sin)):
        M = fn(ang).astype(np.float32)
        M = M.reshape(NPC, NTP, 128, KG, G).transpose(3, 0, 2, 1, 4)
        out[:, :, :, :, i, :] = M.astype(ml_dtypes.bfloat16)
    out = np.ascontiguousarray(out.reshape(KG, NPC, 128, NTP * 512))
    _TAB_CACHE[S] = out
    return out


def _consts():
    ident = np.eye(128, dtype=np.float32)
    ones = np.ones((128, 128), dtype=ml_dtypes.bfloat16)
    c = np.arange(128)
    same = (c[:, None] // 64) == (c[None, :] // 64)
    ang = 2.0 * np.pi * ((c[:, None] % 64) * (c[None, :] % 64) % 64) / 64.0
    chan = np.zeros((128, 2, 128), dtype=np.float32)
    chan[:, 0, :] = np.where(same, np.cos(ang), 0.0)
    chan[:, 1, :] = np.where(same, -np.sin(ang), 0.0)
    return ident, ones, chan.astype(ml_dtypes.bfloat16)


def _colmajor(v, nchunk):
    v = np.asarray(v, dtype=np.float32)
    lead = v.shape[:-1]
    v = v.reshape(lead + (nchunk, 128))
    return np.ascontiguousarray(np.moveaxis(v, -1, 0))


def make_in_map(xs, cs, W, L):
    ident, ones, chan = _consts()
    NS = len(xs)
    cT = np.zeros((128, KC, 4), dtype=np.float32)
    for s in range(NS):
        cT[:, :, s] = np.asarray(cs[s], dtype=np.float32).reshape(KC, 128).T
    m = {
        "xin": np.ascontiguousarray(np.concatenate(xs, axis=0), dtype=np.float32),
        "cT": cT, "ident": ident, "ones": ones, "chan": chan,
        "badaT": _colmajor(W["b_ada"][:L], 48),
        "gvec": _colmajor(np.stack([W["g_pre_mix"][:L], W["g_post_mix"][:L], W["g_pre_ffn"][:L],
                                    W["g_post_ffn"][:L]], axis=0), KC),
        "convp": _colmajor(np.concatenate([np.asarray(W["conv_w"][:L]), np.asarray(W["conv_b"][:L])[:, None, :],
                                           np.asarray(W["g_conv"][:L])[:, None, :],
                                           np.asarray(W["g_fft"][:L])[:, None, :]], axis=1), 4),
    }
    for k in ("w_ada", "w_in", "w_out", "w_gate", "w_up", "w_down"):
        m[k] = np.ascontiguousarray(np.asarray(W[k][:L], dtype=np.float32))
    for S in sorted(set(x.shape[0] for x in xs)):
        m["tab%d" % S] = _dft_table(S)
    return m


_PROG_CACHE = {}


def kernel(x_prompt, x_sample, c_prompt, c_sample, w_ada, b_ada, g_pre_mix, g_post_mix,
           w_in, conv_w, conv_b, g_conv, g_fft, w_out, g_pre_ffn, g_post_ffn,
           w_gate, w_up, w_down):
    NCORE = 8
    L = 4
    W = dict(w_ada=w_ada, b_ada=b_ada, g_pre_mix=g_pre_mix, g_post_mix=g_post_mix, w_in=w_in,
             conv_w=conv_w, conv_b=conv_b, g_conv=g_conv, g_fft=g_fft, w_out=w_out,
             g_pre_ffn=g_pre_ffn, g_post_ffn=g_post_ffn, w_gate=w_gate, w_up=w_up, w_down=w_down)
    W = {k: np.asarray(v) for k, v in W.items()}
    x_prompt = np.asarray(x_prompt)
    x_sample = np.asarray(x_sample)
    c_prompt = np.asarray(c_prompt)
    c_sample = np.asarray(c_sample)
    Ss = x_sample.shape[1]
    Sp = x_prompt.shape[1]
    seqs = (Ss, Ss, Sp)
    key = (seqs, L)
    if key not in _PROG_CACHE:
        _PROG_CACHE[key] = build_program(list(seqs), L)
    nc = _PROG_CACHE[key]
    in_maps = []
    for c in range(NCORE):
        xs = [x_sample[2 * c], x_sample[2 * c + 1], x_prompt[c]]
        cs = [c_sample[2 * c], c_sample[2 * c + 1], c_prompt[c]]
        in_maps.append(make_in_map(xs, cs, W, L))
    res = run_bass_kernel_spmd(nc, in_maps, core_ids=list(range(NCORE)))
    y_prompt = np.empty_like(x_prompt, dtype=np.float32)
    y_sample = np.empty_like(x_sample, dtype=np.float32)
    for c in range(NCORE):
        y = res.results[c]["yout"]
        y_sample[2 * c] = y[0:Ss]
        y_sample[2 * c + 1] = y[Ss:2 * Ss]
        y_prompt[c] = y[2 * Ss:2 * Ss + Sp]
    return (y_prompt, y_sample)
```
